# Optimizing a Trainium2 kernel written in Bass

```python
import jax, jax.numpy as jnp
from jax import lax
import numpy as np

D_MODEL = 4096
BATCH = 1
SEQ = 16384
DEPTH = 4

CHUNK = 64
N_MEM = 256
W_CROSS = D_MODEL // 4
W_CONV = (D_MODEL - W_CROSS) // 2
W_SG = D_MODEL - W_CROSS - W_CONV
CONV_K = 31
CONV_GROUP = 128
CONV_GROUPS = W_CONV // CONV_GROUP
SG_BLOCK = 128
SG_HEAD_DIM = 128
SG_HEADS = W_SG // SG_HEAD_DIM
CROSS_HEADS = 4
CROSS_HEAD_DIM = W_CROSS // CROSS_HEADS
W_IN = 2 * W_CONV + 2 * W_SG + W_CROSS
D_FF = ((8 * D_MODEL + 3 * 256 - 1) // (3 * 256)) * 256
EPS = 1e-6

kernel_name = "hybrid_conv_gmlp_memxattn_trunk"


def _rmsnorm(x, g):
    xf = x.astype(jnp.float32)
    y = xf * lax.rsqrt(jnp.mean(xf * xf, axis=-1, keepdims=True) + EPS)
    return (y * g.astype(jnp.float32)).astype(x.dtype)


def _layernorm(x, g, b):
    xf = x.astype(jnp.float32)
    mu = jnp.mean(xf, axis=-1, keepdims=True)
    xc = xf - mu
    y = xc * lax.rsqrt(jnp.mean(xc * xc, axis=-1, keepdims=True) + EPS)
    return (y * g.astype(jnp.float32) + b.astype(jnp.float32)).astype(x.dtype)


def _causal_depthwise_conv(x, w, b):
    c = x.shape[-1]
    y = lax.conv_general_dilated(
        x, w[:, None, :].astype(x.dtype), window_strides=(1,),
        padding=((CONV_K - 1, 0),), dimension_numbers=("NWC", "WIO", "NWC"),
        feature_group_count=c)
    return y + b


def _conv_module(a, conv_w, conv_b, ln_g, ln_b):
    val, gate = jnp.split(a, 2, axis=-1)
    z = val * jax.nn.sigmoid(gate)
    z = _causal_depthwise_conv(z, conv_w, conv_b)
    return jax.nn.silu(_layernorm(z, ln_g, ln_b))


def _spatial_gating(g, ln_g, ln_b, w_s, b_s):
    bsz, seq, _ = g.shape
    z = jax.nn.gelu(g, approximate=False)
    u, v = jnp.split(z, 2, axis=-1)
    v = _layernorm(v, ln_g, ln_b)
    vb = v.reshape(bsz, seq // SG_BLOCK, SG_BLOCK, SG_HEADS, SG_HEAD_DIM)
    cidx = jnp.arange(SG_BLOCK) // CHUNK
    mask = cidx[None, :] <= cidx[:, None]
    w = jnp.where(mask[None], w_s, jnp.zeros_like(w_s))
    s = jnp.einsum("hij,bnjhd->bnihd", w, vb) + b_s.T[None, None, :, :, None]
    return u * s.reshape(bsz, seq, W_SG)


def _memory_cross_attention(q, mem, mem_g, w_kv):
    bsz, seq, _ = q.shape
    kv = _rmsnorm(mem, mem_g) @ w_kv
    k, v = jnp.split(kv, 2, axis=-1)
    qh = q.reshape(bsz, seq, CROSS_HEADS, CROSS_HEAD_DIM)
    kh = k.reshape(bsz, -1, CROSS_HEADS, CROSS_HEAD_DIM)
    vh = v.reshape(bsz, -1, CROSS_HEADS, CROSS_HEAD_DIM)
    scale = CROSS_HEAD_DIM ** -0.5
    scores = jnp.einsum("bshd,bmhd->bhsm", qh, kh).astype(jnp.float32) * scale
    probs = jax.nn.softmax(scores, axis=-1).astype(vh.dtype)
    o = jnp.einsum("bhsm,bmhd->bshd", probs, vh)
    return o.reshape(bsz, seq, W_CROSS)


def setup_inputs(seed: int = 0) -> dict:
    key = jax.random.key(seed)
    ks = jax.random.split(key, 24)
    f32 = jnp.float32

    def nrm(k, shape, scale):
        return jax.random.normal(k, shape, f32) * scale

    def gain(k, shape):
        return 1.0 + 0.02 * jax.random.normal(k, shape, f32)

    return {
        "x": jax.random.normal(ks[0], (BATCH, SEQ, D_MODEL), f32),
        "mem": jax.random.normal(ks[1], (BATCH, N_MEM, D_MODEL), f32),
        "norm_mix": gain(ks[2], (DEPTH, D_MODEL)),
        "w_in": nrm(ks[3], (DEPTH, D_MODEL, W_IN), D_MODEL ** -0.5),
        "conv_w": nrm(ks[4], (DEPTH, CONV_K, W_CONV), CONV_K ** -0.5),
        "conv_b": nrm(ks[5], (DEPTH, W_CONV), 0.02),
        "conv_ln_g": gain(ks[6], (DEPTH, W_CONV)),
        "conv_ln_b": nrm(ks[7], (DEPTH, W_CONV), 0.02),
        "sg_ln_g": gain(ks[8], (DEPTH, W_SG)),
        "sg_ln_b": nrm(ks[9], (DEPTH, W_SG), 0.02),
        "sg_w": nrm(ks[10], (DEPTH, SG_HEADS, SG_BLOCK, SG_BLOCK), SG_BLOCK ** -0.5),
        "sg_b": gain(ks[11], (DEPTH, SG_HEADS, SG_BLOCK)),
        "mem_norm": gain(ks[12], (DEPTH, D_MODEL)),
        "w_mem_kv": nrm(ks[13], (DEPTH, D_MODEL, 2 * W_CROSS), D_MODEL ** -0.5),
        "out_norm": gain(ks[14], (DEPTH, D_MODEL)),
        "w_out": nrm(ks[15], (DEPTH, D_MODEL, D_MODEL), D_MODEL ** -0.5),
        "norm_ffn": gain(ks[16], (DEPTH, D_MODEL)),
        "w_gate_up": nrm(ks[17], (DEPTH, D_MODEL, 2 * D_FF), D_MODEL ** -0.5),
        "w_down": nrm(ks[18], (DEPTH, D_FF, D_MODEL), D_FF ** -0.5),
        "final_norm": gain(ks[19], (D_MODEL,)),
    }


def reference(x, mem, norm_mix, w_in, conv_w, conv_b, conv_ln_g, conv_ln_b,
              sg_ln_g, sg_ln_b, sg_w, sg_b, mem_norm, w_mem_kv, out_norm, w_out,
              norm_ffn, w_gate_up, w_down, final_norm):
    for l in range(DEPTH):
        h = _rmsnorm(x, norm_mix[l])
        p = h @ w_in[l]
        a = p[..., : 2 * W_CONV]
        g = p[..., 2 * W_CONV: 2 * W_CONV + 2 * W_SG]
        q = p[..., 2 * W_CONV + 2 * W_SG:]
        y_a = _conv_module(a, conv_w[l], conv_b[l], conv_ln_g[l], conv_ln_b[l])
        y_b = _spatial_gating(g, sg_ln_g[l], sg_ln_b[l], sg_w[l], sg_b[l])
        y_c = _memory_cross_attention(q, mem, mem_norm[l], w_mem_kv[l])
        on = out_norm[l]
        y = jnp.concatenate([
            _rmsnorm(y_a, on[:W_CONV]),
            _rmsnorm(y_b, on[W_CONV:W_CONV + W_SG]),
            _rmsnorm(y_c, on[W_CONV + W_SG:]),
        ], axis=-1)
        x = x + y @ w_out[l]
        h2 = _rmsnorm(x, norm_ffn[l])
        gu = h2 @ w_gate_up[l]
        gate, up = jnp.split(gu, 2, axis=-1)
        x = x + (jax.nn.silu(gate) * up) @ w_down[l]
    return _rmsnorm(x, final_norm)
```

```python
import math
from collections import deque
from contextlib import ExitStack

import numpy as np
import concourse.bass as bass
import concourse.mybir as mybir
from concourse.bass_utils import run_bass_kernel_spmd

F32 = mybir.dt.float32
BF16 = mybir.dt.bfloat16
AF = mybir.ActivationFunctionType
ALU = mybir.AluOpType
AX = mybir.AxisListType

GRAN = 256
CAP = 4096
NSLOT = 4
UK = 8
CONVK = 31
EPS = 1e-6


class Cfg:
    def __init__(self, D, WC, WS, WX, XH, NM, FF, DEPTH, OWN, NCORES, T=256, HALO=256):
        self.D, self.WC, self.WS, self.WX, self.XH, self.NM, self.FF = D, WC, WS, WX, XH, NM, FF
        self.DEPTH, self.OWN, self.NCORES, self.T, self.HALO = DEPTH, OWN, NCORES, T, HALO
        self.KC, self.CG, self.SH, self.XC = D // 128, WC // 128, WS // 128, WX // 128
        self.XDC = WX // XH // 128
        self.MC, self.FC = NM // 128, FF // 128
        self.WIN = 2 * WC + 2 * WS + WX
        self.NTOK = OWN + HALO
        self.NT = self.NTOK // T
        self.NB = T // 128
        assert self.NTOK % T == 0 and HALO % T == 0 and NM == T
        assert WC + WS + WX == D and self.CG % 2 == 0
        o = 0
        self.poff = []
        for _ in range(DEPTH):
            d = {}
            for nm, n in (("nm", self.KC), ("on", self.KC), ("nf", self.KC), ("mn", self.KC),
                          ("cw", CONVK * self.CG), ("cb", self.CG), ("clg", self.CG),
                          ("clb", self.CG), ("sg", self.SH), ("sb", self.SH)):
                d[nm] = o
                o += n
            self.poff.append(d)
        self.fn_off = o
        o += self.KC
        self.NP = o + (o % 2)


FULL = Cfg(D=4096, WC=1536, WS=1536, WX=1024, XH=4, NM=256, FF=11008, DEPTH=4, OWN=2048, NCORES=8)


class V:
    __slots__ = ("ap", "keys")

    def __init__(self, ap, keys=()):
        self.ap = ap
        self.keys = tuple(keys)


class Sched:
    ENG = ("pe", "act", "dve", "pool", "sp")

    def __init__(self):
        self.ops = {e: [] for e in self.ENG}
        self.count = {e: 0 for e in self.ENG}
        self.waited = {e: {} for e in self.ENG}
        self.last_write = {}
        self.readers = {}
        self.dma_count = {}
        self.dma_last = {}

    def op(self, eng, fn, reads=(), writes=(), dma_sem=None, ndma=0):
        need = {}

        def add(sp):
            if sp is not None and need.get(sp[0], 0) < sp[1]:
                need[sp[0]] = sp[1]
        for k in reads:
            add(self.last_write.get(k))
        for k in writes:
            add(self.last_write.get(k))
            r = self.readers.get(k)
            if r:
                for s, v in r.items():
                    add((s, v))
        if dma_sem is not None:
            add(self.dma_last.get(dma_sem))
        waits = []
        w = self.waited[eng]
        for s, v in need.items():
            if s == "pe" and eng == "pe":
                continue
            if w.get(s, 0) < v:
                waits.append((s, v))
                w[s] = v
        if dma_sem is None:
            self.count[eng] += 1
            sp = (eng, self.count[eng])
        else:
            self.dma_count[dma_sem] = self.dma_count.get(dma_sem, 0) + 16 * ndma
            sp = (dma_sem, self.dma_count[dma_sem])
            self.dma_last[dma_sem] = sp
        self.ops[eng].append((waits, fn, sp, dma_sem is not None))
        for k in reads:
            r = self.readers.setdefault(k, {})
            if r.get(sp[0], 0) < sp[1]:
                r[sp[0]] = sp[1]
        for k in writes:
            self.last_write[k] = sp
            self.readers[k] = {}
        return sp

    def final_wait(self, eng, sps):
        need = {}
        for s, v in sps:
            need[s] = max(need.get(s, 0), v)
        self.ops[eng].append((list(need.items()), None, None, False))

    def emit(self, nc):
        sems = {}
        names = list(self.ENG) + sorted(self.dma_count.keys())
        with ExitStack() as st:
            for n in names:
                sems[n] = st.enter_context(nc.semaphore("s_" + n))
            block = st.enter_context(nc.Block())

            def run(eng, e):
                for waits, fn, sp, is_dma in self.ops[eng]:
                    for s, v in waits:
                        e.wait_ge(sems[s], v)
                    if fn is None:
                        continue
                    r = fn(e)
                    if is_dma:
                        for ins in r:
                            ins.then_inc(sems[sp[0]], 16)
                    else:
                        r.then_inc(sems[eng], 1)

            @block.tensor
            def _(e):
                run("pe", e)

            @block.scalar
            def _(e):
                run("act", e)

            @block.vector
            def _(e):
                run("dve", e)

            @block.gpsimd
            def _(e):
                run("pool", e)

            @block.sync
            def _(e):
                run("sp", e)


def build(c):
    nc = bass.Bass("TRN2", target_bir_lowering=False)
    T, NB, KC, CG, SH, XC, XDC, MC, FC, NM = c.T, c.NB, c.KC, c.CG, c.SH, c.XC, c.XDC, c.MC, c.FC, c.NM
    D, WC, WS, WX, FF, DEPTH = c.D, c.WC, c.WS, c.WX, c.FF, c.DEPTH

    def dram(name, shape, dt=F32, kind="ExternalInput"):
        return nc.dram_tensor(name, list(shape), dt, kind=kind).ap()

    xsT = dram("xsT", [D, c.NTOK])
    memT = dram("memT", [D, NM])
    params_d = dram("params", [128, c.NP])
    cbs_d = dram("cbs", [128, DEPTH * SH * 128])
    sgwT_d = dram("sgwT", [DEPTH * 128, SH * 128])
    ident_d = dram("ident", [128, 128])
    zmask_d = dram("zmask", [128, T])
    w_in = dram("w_in", [DEPTH * D, c.WIN])
    w_kv = dram("w_mem_kv", [DEPTH * D, 2 * WX])
    w_out = dram("w_out", [DEPTH * D, D])
    w_gu = dram("w_gate_up", [DEPTH * D, 2 * FF])
    w_dn = dram("w_down", [DEPTH * FF, D])
    outT = dram("outT", [D, c.OWN], kind="ExternalOutput")
    KVW = XC * NM + MC * WX
    kvscr = dram("kvscr", [DEPTH * 128, KVW], BF16, kind="Internal")

    S = Sched()
    steps = []
    bg = deque()

    alloc_off = [0]
    bufs = []

    class Buf:
        def __init__(self, name, n, w, dt):
            self.name, self.n, self.w, self.dt = name, n, w, dt
            self.esz = 4 if dt == F32 else 2
            self.nbytes = n * w * self.esz
            assert self.nbytes % 4 == 0
            self.off = alloc_off[0]
            alloc_off[0] += (self.nbytes + GRAN - 1) // GRAN * GRAN
            bufs.append(self)

        def full(self):
            a = arena[:, self.off // 4:(self.off + self.nbytes) // 4]
            if self.dt != F32:
                a = a.bitcast(self.dt)
            return a

        def all(self):
            return V(self.full(), [("sb", g) for g in range(self.off // GRAN, (self.off + self.nbytes - 1) // GRAN + 1)])

        def v(self, i=0, c0=0, c1=None, n=1):
            w = self.w
            if c1 is None:
                c1 = w
            f = self.full()
            if n == 1:
                ap = f[:, i * w + c0:i * w + c1]
            else:
                ap = f[:, i * w:(i + n) * w].rearrange("p (n w) -> p n w", w=w)[:, :, c0:c1]
            b0 = self.off + (i * w + c0) * self.esz
            b1 = self.off + ((i + n - 1) * w + c1) * self.esz
            return V(ap, [("sb", g) for g in range(b0 // GRAN, (b1 - 1) // GRAN + 1)])

    ZW = CONVK - 1 + T
    xT = Buf("xT", KC, T, F32)
    bufA = Buf("bufA", KC, T, BF16)
    ring = Buf("ring", NSLOT, CAP, BF16)
    prm = Buf("prm", 1, c.NP, F32)
    zhalo = Buf("zhalo", DEPTH * CG, CONVK - 1 + 2, F32)
    kvbuf = Buf("kvbuf", 1, KVW, BF16)
    ones = Buf("ones", 1, 128, BF16)
    ident = Buf("ident", 1, 128, BF16)
    zmask = Buf("zmask", 1, T, F32)
    epsb = Buf("epsb", 1, 2, F32)
    zfull = Buf("zfull", CG, ZW, F32)
    sig = Buf("sig", 2, T, F32)
    acc = Buf("acc", 2, T, F32)
    cA = Buf("cA", CG, T, BF16)
    vt = Buf("vt", NB, WS, BF16)
    vtmp = Buf("vtmp", 2, WS, F32)
    vsq = Buf("vsq", 1, WS, F32)
    ub = Buf("ub", SH, T, BF16)
    stmp = Buf("stmp", 2, T, F32)
    WmT = Buf("WmT", SH, 128, BF16)
    Eb = Buf("Eb", SH, 128, F32)
    qT = Buf("qT", XC, T, BF16)
    oX = Buf("oX", XC, T, F32)
    yX = Buf("yX", XC, T, BF16)
    ebuf = Buf("ebuf", 2, NM, F32)
    pbuf = Buf("pbuf", 2, NM, BF16)
    pT = Buf("pT", 2 * MC, T, BF16)
    sqt = Buf("sqt", 2, T, BF16)
    cbt = Buf("cbt", 2, T, BF16)
    meanb = Buf("meanb", 1, T, F32)
    varb = Buf("varb", 1, T, F32)
    rstdb = Buf("rstdb", 1, T, F32)
    small = Buf("small", 1, 64, F32)
    outst = Buf("outst", 2, T, F32)
    actb = Buf("actb", 2 * 4, T, BF16)
    sgb = Buf("sgb", 4, T, F32)
    total_bytes = alloc_off[0]
    assert total_bytes <= 206 * 1024, total_bytes

    st = ExitStack()
    arena = st.enter_context(nc.sbuf_tensor("arena", [128, total_bytes // 4], F32))
    psum = st.enter_context(nc.psum_tensor("psum", [128, 8, 512], F32))

    def bank(b, n0=0, n1=None):
        if n1 is None:
            n1 = T
        return V(psum[:, b, n0:n1], [("ps", b)])

    def bank_bf(b, n0, n1):
        return V(psum[:, b, :].bitcast(BF16)[:, n0:n1], [("ps", b)])

    def P(l, name, i=0, n=1):
        o = c.poff[l][name] + i
        return prm.v(0, o, o + n)

    def act(out, in_, func, bias=None, scale=None):
        kw = {}
        r = list(in_.keys)
        if isinstance(bias, V):
            kw["bias"] = bias.ap
            r += bias.keys
        elif bias is not None:
            kw["bias"] = bias
        if isinstance(scale, V):
            kw["scale"] = scale.ap
            r += scale.keys
        elif scale is not None:
            kw["scale"] = scale
        S.op("act", lambda e: e.activation(out=out.ap, in_=in_.ap, func=func, **kw), r, out.keys)

    def tt(out, a, b, op):
        S.op("dve", lambda e: e.tensor_tensor(out=out.ap, in0=a.ap, in1=b.ap, op=op),
             a.keys + b.keys, out.keys)

    def ts(out, a, s1, op0, s2=None, op1=None):
        r = list(a.keys)
        a1 = s1
        a2 = s2
        if isinstance(s1, V):
            a1 = s1.ap
            r += s1.keys
        if isinstance(s2, V):
            a2 = s2.ap
            r += s2.keys
        if op1 is None:
            S.op("dve", lambda e: e.tensor_scalar(out=out.ap, in0=a.ap, scalar1=a1, scalar2=None, op0=op0),
                 r, out.keys)
        else:
            S.op("dve", lambda e: e.tensor_scalar(out=out.ap, in0=a.ap, scalar1=a1, scalar2=a2, op0=op0, op1=op1),
                 r, out.keys)

    def stt(out, a, s, b, op0, op1):
        r = list(a.keys) + list(b.keys)
        sa = s
        if isinstance(s, V):
            sa = s.ap
            r += s.keys
        S.op("dve", lambda e: e.scalar_tensor_tensor(out=out.ap, in0=a.ap, scalar=sa, in1=b.ap, op0=op0, op1=op1),
             r, out.keys)

    def vcopy(out, a):
        S.op("dve", lambda e: e.tensor_copy(out=out.ap, in_=a.ap), a.keys, out.keys)

    def recip(out, a):
        S.op("dve", lambda e: e.reciprocal(out=out.ap, in_=a.ap), a.keys, out.keys)

    def rmax(out, a):
        S.op("dve", lambda e: e.reduce_max(out=out.ap, in_=a.ap, axis=AX.X), a.keys, out.keys)

    def rsum(out, a):
        S.op("dve", lambda e: e.reduce_sum(out=out.ap, in_=a.ap, axis=AX.X), a.keys, out.keys)

    def memset(out, val):
        S.op("dve", lambda e: e.memset(out.ap, val), (), out.keys)

    def mm(group, reads, writes):
        def fn(e):
            ins = None
            for o, l, r, s0, s1 in group:
                ins = e.matmul(o, l, r, start=s0, stop=s1)
            return ins
        S.op("pe", fn, reads, writes)

    def transpose(out, in_):
        idv = ident.v()
        S.op("pe", lambda e: e.transpose(out.ap, in_.ap, idv.ap), in_.keys + idv.keys, out.keys)

    def dma(eng, sem, pairs, reads, writes):
        def fn(e):
            return [e.dma_start(out=o, in_=i) for o, i in pairs]
        return S.op(eng, fn, reads, writes, dma_sem=sem, ndma=len(pairs))

    units = []

    class Unit:
        def __init__(self, W2d, row0, kn, runs):
            self.W, self.row0, self.kn, self.runs = W2d, row0, kn, runs
            self.ncols = sum(n for _, n in runs)
            assert self.kn * self.ncols <= CAP
            self.idx = len(units)
            self.slot = self.idx % NSLOT
            units.append(self)

        def view(self):
            f = ring.full()
            return f[:, self.slot * CAP:self.slot * CAP + self.kn * self.ncols].rearrange(
                "p (k c) -> p k c", c=self.ncols)

        def keys(self):
            return ring.v(self.slot).keys

        def issue(self):
            sv = self.view()
            pairs = []
            o = 0
            for c0, n in self.runs:
                src = self.W[self.row0:self.row0 + self.kn * 128, c0:c0 + n].rearrange("(k p) c -> p k c", p=128)
                pairs.append((sv[:, :, o:o + n], src))
                o += n
            dma("pool", "ring%d" % self.slot, pairs, (), self.keys())

    def add_step(fn, unit=None):
        steps.append((unit, fn))

    def drain_bg(n):
        while bg and n > 0:
            bg.popleft()()
            n -= 1

    BGRATE = [4]

    def proj(W2d, row_kc0, KCn, groups, rhs, evac, N, token_major=False, lhs_blocks=None):
        for gi, runs in enumerate(groups):
            ncols = sum(n for _, n in runs)
            nb = (ncols // 128) if not token_major else lhs_blocks
            nun = (KCn + UK - 1) // UK
            for u in range(nun):
                k0 = u * UK
                kn = min(UK, KCn - k0)
                un = Unit(W2d, (row_kc0 + k0) * 128, kn, runs)

                def fn(un=un, k0=k0, kn=kn, gi=gi, nb=nb, ncols=ncols, last=(u == nun - 1)):
                    sv = un.view()
                    for j in range(nb):
                        grp = []
                        rk = list(un.keys())
                        for kk in range(kn):
                            kc = k0 + kk
                            r = rhs(kc)
                            rk += r.keys
                            if not token_major:
                                grp.append((psum[:, j, 0:N], sv[:, kk, j * 128:(j + 1) * 128], r.ap,
                                            kc == 0, kc == KCn - 1))
                            else:
                                grp.append((psum[:, j, 0:ncols], r.ap[:, j * 128:(j + 1) * 128], sv[:, kk, :],
                                            kc == 0, kc == KCn - 1))
                        mm(grp, rk, [("ps", j)])
                        if last:
                            evac(gi, j, V(psum[:, j, 0:(N if not token_major else ncols)], [("ps", j)]))
                    drain_bg(BGRATE[0])
                add_step(fn, un)

    AUX = (4, 5)
    MISC = 6
    TRB = 7
    aux_i = [0]

    def next_aux():
        aux_i[0] ^= 1
        return AUX[aux_i[0]]

    def ones_sum(srcs, width, dst_rstd):
        b = next_aux()
        n = len(srcs)
        N = srcs[0].ap.shape[-1]
        for i, s in enumerate(srcs):
            q = sqt.v(i % 2, 0, N)
            act(q, s, AF.Square)
            mm([(psum[:, b, 0:N], ones.v().ap, q.ap, i == 0, i == n - 1)], q.keys + ones.v().keys, [("ps", b)])
        act(dst_rstd, V(psum[:, b, 0:N], [("ps", b)]), AF.Sqrt, bias=epsb.v(0, 0, 1), scale=1.0 / width)
        recip(dst_rstd, dst_rstd)

    def rms_apply(srcs, gains, dsts, width, rstd):
        ones_sum(srcs, width, rstd)
        for s, g, d in zip(srcs, gains, dsts):
            stt(d, s, g, rstd, ALU.mult, ALU.mult)

    dma("sp", "ld0", [(prm.v().ap, params_d[:, :])], (), prm.v().keys)
    dma("sp", "ld0", [(zmask.v().ap, zmask_d[:, :])], (), zmask.v().keys)
    dma("pool", "ld1", [(ident.v().ap, ident_d[:, :])], (), ident.v().keys)
    memset(ones.v(), 1.0)
    memset(epsb.v(), EPS)
    memset(zhalo.all(), 0.0)

    def load_cols(dst_buf, src2d, col0, ncol, sem):
        for k0 in range(0, KC, 8):
            kn = min(8, KC - k0)
            d = dst_buf.v(k0, 0, ncol, n=kn)
            src = src2d[k0 * 128:(k0 + kn) * 128, col0:col0 + ncol].rearrange("(k p) c -> p k c", p=128)
            dap = d.ap if kn > 1 else d.ap
            if kn == 1:
                src = src2d[k0 * 128:(k0 + 1) * 128, col0:col0 + ncol]
            dma("sp", sem, [(dap, src)], (), d.keys)

    def phase0():
        load_cols(xT, memT, 0, NM, "xld")
        for l in range(DEPTH):
            def f(l=l):
                rms_apply([xT.v(k) for k in range(KC)], [P(l, "mn", k) for k in range(KC)],
                          [bufA.v(k) for k in range(KC)], D, rstdb.v())
            add_step(f)
            kgroups = [[(g * 512, min(512, WX - g * 512))] for g in range((WX + 511) // 512)]

            def evk(gi, j, bv, l=l):
                xc = gi * 4 + j
                act(kvbuf.v(0, xc * NM, (xc + 1) * NM), bv, AF.Copy)
            proj(w_kv, l * KC, KC, kgroups, lambda kc: bufA.v(kc), evk, NM)
            vgroups = [[(WX + g * 512, min(512, WX - g * 512))] for g in range((WX + 511) // 512)]

            def evv(gi, b, bv, l=l):
                o = XC * NM + b * WX + gi * 512
                act(kvbuf.v(0, o, o + bv.ap.shape[-1]), bv, AF.Copy)
            proj(w_kv, l * KC, KC, vgroups, lambda kc: bufA.v(kc), evv, NM, token_major=True, lhs_blocks=MC)

            def st_(l=l):
                dma("sp", "kvst", [(kvscr[l * 128:(l + 1) * 128, :], kvbuf.v().ap)], kvbuf.v().keys, [("kvscr", l)])
            add_step(st_)

    def layer(t, l, last_tile_layer):
        first_tile = (t == 0)

        def prologue():
            dma("sp", "kvld", [(kvbuf.v().ap, kvscr[l * 128:(l + 1) * 128, :])], [("kvscr", l)], kvbuf.v().keys)
            dma("pool", "ld1", [(WmT.all().ap, sgwT_d[l * 128:(l + 1) * 128, :])], (), WmT.all().keys)
            dma("sp", "ld0", [(Eb.all().ap, cbs_d[:, l * SH * 128:(l + 1) * SH * 128])], (), Eb.all().keys)
            wv = WmT.v(0, 0, 64, n=SH)
            S.op("dve", lambda e: e.memset(wv.ap[64:128], 0.0), (), wv.keys)
            for h0 in range(0, SH, 4):
                hn = min(4, SH - h0)
                wm = WmT.v(h0, n=hn)
                mm([(psum[:, MISC, 0:hn * 128], ones.v().ap, WmT.full()[:, h0 * 128:(h0 + hn) * 128], True, True)],
                   wm.keys + ones.v().keys, [("ps", MISC)])
                for h in range(h0, h0 + hn):
                    stt(Eb.v(h), V(psum[:, MISC, (h - h0) * 128:(h - h0 + 1) * 128], [("ps", MISC)]),
                        P(l, "sb", h), Eb.v(h), ALU.mult, ALU.add)
            rms_apply([xT.v(k) for k in range(KC)], [P(l, "nm", k) for k in range(KC)],
                      [bufA.v(k) for k in range(KC)], D, rstdb.v())
        add_step(prologue)

        hrhs = lambda kc: bufA.v(kc)

        def conv_chunk_ops(cc):
            zf = lambda a, b: zfull.v(cc, a, b)
            a_ = acc.v(cc % 2)
            ops = []
            ops.append(lambda: vcopy(zf(0, CONVK - 1), zhalo.v(l * CG + cc, 0, CONVK - 1)))
            ops.append(lambda: ts(a_, zf(0, T), P(l, "cw", 0 * CG + cc), ALU.mult, P(l, "cb", cc), ALU.add))
            for k in range(1, CONVK):
                ops.append(lambda k=k: stt(a_, zf(k, k + T), P(l, "cw", k * CG + cc), a_, ALU.mult, ALU.add))
            ops.append(lambda: vcopy(zhalo.v(l * CG + cc, 0, CONVK - 1), zf(T, T + CONVK - 1)))
            ops.append(lambda: act(cA.v(cc), a_, AF.Copy))
            return ops

        def evA(gi, j, bv):
            if j < 2:
                return
            cc = 2 * gi + (j - 2)
            sg_ = sig.v(cc % 2)
            act(sg_, bv, AF.Sigmoid)
            zt = zfull.v(cc, CONVK - 1, CONVK - 1 + T)
            tt(zt, V(psum[:, j - 2, 0:T], [("ps", j - 2)]), sg_, ALU.mult)
            if first_tile:
                tt(zt, zt, zmask.v(), ALU.mult)
            if cc % 2 == 1:
                o0 = conv_chunk_ops(cc - 1)
                o1 = conv_chunk_ops(cc)
                for a, b in zip(o0, o1):
                    bg.append(a)
                    bg.append(b)

        agroups = [[(g * 256, 256), (WC + g * 256, 256)] for g in range(CG // 2)]
        proj(w_in, l * KC, KC, agroups, hrhs, evA, T)

        def evGv(gi, b, bv):
            o = gi * 512
            w = bv.ap.shape[-1]
            act(vtmp.v(b % 2, o, o + w), bv, AF.Gelu)
            if o + w == WS:
                def lnv(b=b):
                    src = vtmp.v(b % 2)
                    s1 = small.v(0, 0 + b * 8, 1 + b * 8)
                    s2 = small.v(0, 1 + b * 8, 2 + b * 8)
                    m_ = small.v(0, 2 + b * 8, 3 + b * 8)
                    v_ = small.v(0, 3 + b * 8, 4 + b * 8)
                    rsum(s1, src)
                    ts(m_, s1, 1.0 / WS, ALU.mult)
                    ts(src, src, m_, ALU.subtract)
                    tt(vsq.v(), src, src, ALU.mult)
                    rsum(s2, vsq.v())
                    ts(v_, s2, 1.0 / WS, ALU.mult, EPS, ALU.add)
                    act(v_, v_, AF.Sqrt)
                    recip(v_, v_)
                    ts(vt.v(b), src, v_, ALU.mult)
                bg.append(lnv)

        vgroups = [[(2 * WC + WS + g * 512, min(512, WS - g * 512))] for g in range((WS + 511) // 512)]
        proj(w_in, l * KC, KC, vgroups, hrhs, evGv, T, token_major=True, lhs_blocks=NB)

        def evGu(gi, j, bv):
            h = gi * 4 + j
            act(ub.v(h), bv, AF.Gelu)

            def mix(h=h):
                grp = []
                rk = list(WmT.v(h).keys)
                for b in range(NB):
                    nv = vt.v(b, h * 128, (h + 1) * 128)
                    rk += nv.keys
                    grp.append((psum[:, MISC, b * 128:(b + 1) * 128], nv.ap, WmT.v(h).ap, True, True))
                mm(grp, rk, [("ps", MISC)])
                s_ = stmp.v(h % 2)
                for b in range(NB):
                    stt(stmp.v(h % 2, b * 128, (b + 1) * 128),
                        V(psum[:, MISC, b * 128:(b + 1) * 128], [("ps", MISC)]),
                        P(l, "sg", h), Eb.v(h), ALU.mult, ALU.add)
                tt(ub.v(h), ub.v(h), s_, ALU.mult)
            bg.append(mix)

        ugroups = [[(2 * WC + g * 512, min(512, WS - g * 512))] for g in range((WS + 511) // 512)]
        proj(w_in, l * KC, KC, ugroups, hrhs, evGu, T)

        def a_post():
            drain_bg(1 << 30)
            b1 = next_aux()
            b2 = next_aux()
            for cc in range(CG):
                q = sqt.v(cc % 2)
                act(q, cA.v(cc), AF.Square)
                mm([(psum[:, b1, 0:T], ones.v().ap, cA.v(cc).ap, cc == 0, cc == CG - 1)],
                   cA.v(cc).keys + ones.v().keys, [("ps", b1)])
                mm([(psum[:, b2, 0:T], ones.v().ap, q.ap, cc == 0, cc == CG - 1)],
                   q.keys + ones.v().keys, [("ps", b2)])
            ts(meanb.v(), bank(b1), 1.0 / WC, ALU.mult)
            tt(varb.v(), meanb.v(), meanb.v(), ALU.mult)
            stt(varb.v(), bank(b2), 1.0 / WC, varb.v(), ALU.mult, ALU.subtract)
            act(varb.v(), varb.v(), AF.Sqrt, bias=epsb.v(0, 0, 1), scale=1.0)
            recip(varb.v(), varb.v())
            for cc in range(CG):
                a_ = acc.v(cc % 2)
                tt(a_, cA.v(cc), meanb.v(), ALU.subtract)
                tt(a_, a_, varb.v(), ALU.mult)
                act(a_, a_, AF.Silu, bias=P(l, "clb", cc), scale=P(l, "clg", cc))
                act(cA.v(cc), a_, AF.Copy)
            rms_apply([cA.v(cc) for cc in range(CG)], [P(l, "on", cc) for cc in range(CG)],
                      [cA.v(cc) for cc in range(CG)], WC, rstdb.v())
        add_step(a_post)

        def ev_res(gi, j, bv):
            m = gi * 4 + j
            tt(xT.v(m), xT.v(m), bv, ALU.add)

        ogroups = [[(g * 512, min(512, D - g * 512))] for g in range((D + 511) // 512)]
        proj(w_out, l * KC, CG, ogroups, lambda kc: cA.v(kc), ev_res, T)

        scale = float((WX // c.XH) ** -0.5)

        def evQ(gi, j, bv):
            xc = gi * 4 + j
            act(qT.v(xc), bv, AF.Copy)
            if (xc + 1) % XDC == 0:
                h = xc // XDC
                for b in range(NB):
                    def att(h=h, b=b):
                        grp = []
                        rk = list(kvbuf.v().keys)
                        for d_ in range(XDC):
                            q_ = qT.v(h * XDC + d_, b * 128, (b + 1) * 128)
                            rk += q_.keys
                            kv_ = kvbuf.v(0, (h * XDC + d_) * NM, (h * XDC + d_ + 1) * NM)
                            grp.append((psum[:, MISC, 0:NM], q_.ap, kv_.ap, d_ == 0, d_ == XDC - 1))
                        mm(grp, rk, [("ps", MISC)])
                        sc = V(psum[:, MISC, 0:NM], [("ps", MISC)])
                        i_ = (h * NB + b) % 2
                        mx = small.v(0, 32 + i_ * 4, 33 + i_ * 4)
                        sm = small.v(0, 34 + i_ * 4, 35 + i_ * 4)
                        rmax(mx, sc)
                        ts(mx, mx, -scale, ALU.mult)
                        e_ = ebuf.v(i_)
                        act(e_, sc, AF.Exp, bias=mx, scale=scale)
                        rsum(sm, e_)
                        recip(sm, sm)
                        p_ = pbuf.v(i_)
                        ts(p_, e_, sm, ALU.mult)
                        for mc in range(MC):
                            tb = bank_bf(TRB, mc * 128, (mc + 1) * 128)
                            transpose(tb, pbuf.v(i_, mc * 128, (mc + 1) * 128))
                            act(pT.v((h % 2) * MC + mc, b * 128, (b + 1) * 128), tb, AF.Copy)
                    bg.append(att)

                def av(h=h):
                    for d_ in range(XDC):
                        grp = []
                        rk = list(kvbuf.v().keys)
                        bk = AUX[d_ % 2]
                        for mc in range(MC):
                            p_ = pT.v((h % 2) * MC + mc)
                            rk += p_.keys
                            o = XC * NM + mc * WX + (h * XDC + d_) * 128
                            grp.append((psum[:, bk, 0:T], kvbuf.v(0, o, o + 128).ap, p_.ap, mc == 0, mc == MC - 1))
                        mm(grp, rk, [("ps", bk)])
                        act(oX.v(h * XDC + d_), bank(bk), AF.Copy)
                bg.append(av)

        qgroups = [[(2 * WC + 2 * WS + g * 512, min(512, WX - g * 512))] for g in range((WX + 511) // 512)]
        proj(w_in, l * KC, KC, qgroups, hrhs, evQ, T)

        def g_post():
            drain_bg(1 << 30)
            rms_apply([ub.v(h) for h in range(SH)], [P(l, "on", CG + h) for h in range(SH)],
                      [ub.v(h) for h in range(SH)], WS, rstdb.v())
        add_step(g_post)
        proj(w_out, l * KC + CG, SH, ogroups, lambda kc: ub.v(kc), ev_res, T)

        def x_post():
            drain_bg(1 << 30)
            rms_apply([oX.v(k) for k in range(XC)], [P(l, "on", CG + SH + k) for k in range(XC)],
                      [yX.v(k) for k in range(XC)], WX, rstdb.v())
        add_step(x_post)
        proj(w_out, l * KC + CG + SH, XC, ogroups, lambda kc: yX.v(kc), ev_res, T)

        def ffn_pro():
            rms_apply([xT.v(k) for k in range(KC)], [P(l, "nf", k) for k in range(KC)],
                      [bufA.v(k) for k in range(KC)], D, rstdb.v())
        add_step(ffn_pro)
        nfg = (FC + 3) // 4
        for fg in range(nfg):
            f0 = fg * 4
            fn_ = min(4, FC - f0)
            half = fg % 2

            def evg(gi, j, bv):
                act(sgb.v(j), bv, AF.Silu)

            def evu(gi, j, bv, half=half):
                tt(actb.v(half * 4 + j), bv, sgb.v(j), ALU.mult)
            proj(w_gu, l * KC, KC, [[(f0 * 128, fn_ * 128)]], hrhs, evg, T)
            proj(w_gu, l * KC, KC, [[(FF + f0 * 128, fn_ * 128)]], hrhs, evu, T)
            proj(w_dn, l * FC + f0, fn_, ogroups, lambda kc, half=half: actb.v(half * 4 + kc), ev_res, T)

        if last_tile_layer:
            def fin():
                ones_sum([xT.v(k) for k in range(KC)], D, rstdb.v())
                c0 = t * T - c.HALO
                for k in range(KC):
                    o_ = outst.v(k % 2)
                    stt(o_, xT.v(k), prm.v(0, c.fn_off + k, c.fn_off + k + 1), rstdb.v(), ALU.mult, ALU.mult)
                    sp = dma("sp", "ost%d" % (k % 2), [(outT[k * 128:(k + 1) * 128, c0:c0 + T], o_.ap)], o_.keys, ())
                    out_sps.append(sp)
            add_step(fin)

    out_sps = []

    phase0()
    for t in range(c.NT):
        def ld(t=t):
            load_cols(xT, xsT, t * T, T, "xld")
        add_step(ld)
        for l in range(DEPTH):
            layer(t, l, (l == DEPTH - 1) and (t * T >= c.HALO))

    issued = 0
    for un, fn in steps:
        if un is not None:
            while issued < len(units) and issued <= un.idx + NSLOT - 1:
                units[issued].issue()
                issued += 1
        fn()
    drain_bg(1 << 30)
    S.final_wait("sp", out_sps)
    S.emit(nc)
    st.close()
    return nc


def host_inputs(c, inp):
    D, T = c.D, c.T
    x = np.asarray(inp["x"], np.float32)[0]
    mem = np.asarray(inp["mem"], np.float32)[0]

    def pm(v):
        return np.ascontiguousarray(np.asarray(v, np.float32).reshape(-1, 128).T)
    params = np.zeros((128, c.NP), np.float32)
    for l in range(c.DEPTH):
        o = c.poff[l]
        params[:, o["nm"]:o["nm"] + c.KC] = pm(inp["norm_mix"][l])
        params[:, o["on"]:o["on"] + c.KC] = pm(inp["out_norm"][l])
        params[:, o["nf"]:o["nf"] + c.KC] = pm(inp["norm_ffn"][l])
        params[:, o["mn"]:o["mn"] + c.KC] = pm(inp["mem_norm"][l])
        cw = np.asarray(inp["conv_w"][l], np.float32)
        for k in range(CONVK):
            params[:, o["cw"] + k * c.CG:o["cw"] + (k + 1) * c.CG] = pm(cw[k])
        params[:, o["cb"]:o["cb"] + c.CG] = pm(inp["conv_b"][l])
        params[:, o["clg"]:o["clg"] + c.CG] = pm(inp["conv_ln_g"][l])
        params[:, o["clb"]:o["clb"] + c.CG] = pm(inp["conv_ln_b"][l])
        params[:, o["sg"]:o["sg"] + c.SH] = pm(inp["sg_ln_g"][l])
        params[:, o["sb"]:o["sb"] + c.SH] = pm(inp["sg_ln_b"][l])
    params[:, c.fn_off:c.fn_off + c.KC] = pm(inp["final_norm"])
    sgb = np.asarray(inp["sg_b"], np.float32).reshape(1, -1)
    cbs = np.ascontiguousarray(np.broadcast_to(sgb, (128, sgb.shape[1])))
    sgw = np.asarray(inp["sg_w"], np.float32)
    sgwT = np.ascontiguousarray(sgw.transpose(0, 3, 1, 2)).reshape(c.DEPTH * 128, c.SH * 128)
    ident = np.eye(128, dtype=np.float32)
    memT = np.ascontiguousarray(mem.T)
    shared = {
        "memT": memT, "params": params, "cbs": cbs, "sgwT": sgwT, "ident": ident,
        "w_in": np.asarray(inp["w_in"], np.float32).reshape(c.DEPTH * D, c.WIN),
        "w_mem_kv": np.asarray(inp["w_mem_kv"], np.float32).reshape(c.DEPTH * D, 2 * c.WX),
        "w_out": np.asarray(inp["w_out"], np.float32).reshape(c.DEPTH * D, D),
        "w_gate_up": np.asarray(inp["w_gate_up"], np.float32).reshape(c.DEPTH * D, 2 * c.FF),
        "w_down": np.asarray(inp["w_down"], np.float32).reshape(c.DEPTH * c.FF, D),
    }
    maps = []
    for core in range(c.NCORES):
        s = core * c.OWN
        xs = np.zeros((c.NTOK, D), np.float32)
        lo = s - c.HALO
        if lo >= 0:
            xs[:] = x[lo:s + c.OWN]
        else:
            xs[c.HALO:] = x[s:s + c.OWN]
        zm = np.ones((128, T), np.float32)
        if core == 0:
            zm[:] = 0.0
        m = dict(shared)
        m["xsT"] = np.ascontiguousarray(xs.T)
        m["zmask"] = zm
        maps.append(m)
    return maps


def run(c, inp):
    nc = build(c)
    maps = host_inputs(c, inp)
    res = run_bass_kernel_spmd(nc, maps, core_ids=list(range(c.NCORES)))
    outs = [np.asarray(r["outT"], np.float32).T for r in res.results]
    return np.concatenate(outs, axis=0)[None]


def kernel(**inputs):
    return run(FULL, inputs)
```

```python
import math
from collections import deque
from contextlib import ExitStack

import numpy as np
import concourse.bass as bass
import concourse.mybir as mybir
from concourse.bass_utils import run_bass_kernel_spmd

F32 = mybir.dt.float32
BF16 = mybir.dt.bfloat16
AF = mybir.ActivationFunctionType
ALU = mybir.AluOpType
AX = mybir.AxisListType

GRAN = 256
CAP = 4096
NSLOT = 4
UK = 8
CONVK = 31
EPS = 1e-6


class Cfg:
    def __init__(self, D, WC, WS, WX, XH, NM, FF, DEPTH, OWN, NCORES, T=256, HALO=256):
        self.D, self.WC, self.WS, self.WX, self.XH, self.NM, self.FF = D, WC, WS, WX, XH, NM, FF
        self.DEPTH, self.OWN, self.NCORES, self.T, self.HALO = DEPTH, OWN, NCORES, T, HALO
        self.KC, self.CG, self.SH, self.XC = D // 128, WC // 128, WS // 128, WX // 128
        self.XDC = WX // XH // 128
        self.MC, self.FC = NM // 128, FF // 128
        self.WIN = 2 * WC + 2 * WS + WX
        self.NTOK = OWN + HALO
        self.NT = self.NTOK // T
        self.NB = T // 128
        assert self.NTOK % T == 0 and HALO % T == 0 and NM == T
        assert WC + WS + WX == D and self.CG % 2 == 0
        o = 0
        self.poff = []
        for _ in range(DEPTH):
            d = {}
            for nm, n in (("nm", self.KC), ("on", self.KC), ("nf", self.KC), ("mn", self.KC),
                          ("cw", CONVK * self.CG), ("cb", self.CG), ("clg", self.CG),
                          ("clb", self.CG), ("sg", self.SH), ("sb", self.SH)):
                d[nm] = o
                o += n
            self.poff.append(d)
        self.fn_off = o
        o += self.KC
        self.NP = o + (o % 2)


FULL = Cfg(D=4096, WC=1536, WS=1536, WX=1024, XH=4, NM=256, FF=11008, DEPTH=4, OWN=2048, NCORES=8)


class V:
    __slots__ = ("ap", "keys")

    def __init__(self, ap, keys=()):
        self.ap = ap
        self.keys = tuple(keys)


class Sched:
    ENG = ("pe", "act", "dve", "pool", "sp")

    def __init__(self):
        self.ops = {e: [] for e in self.ENG}
        self.count = {e: 0 for e in self.ENG}
        self.waited = {e: {} for e in self.ENG}
        self.last_write = {}
        self.readers = {}
        self.dma_count = {}
        self.dma_last = {}

    def op(self, eng, fn, reads=(), writes=(), dma_sem=None, ndma=0):
        need = {}

        def add(sp):
            if sp is not None and need.get(sp[0], 0) < sp[1]:
                need[sp[0]] = sp[1]
        for k in reads:
            add(self.last_write.get(k))
        for k in writes:
            add(self.last_write.get(k))
            r = self.readers.get(k)
            if r:
                for s, v in r.items():
                    add((s, v))
        if dma_sem is not None:
            add(self.dma_last.get(dma_sem))
        waits = []
        w = self.waited[eng]
        for s, v in need.items():
            if s == "pe" and eng == "pe":
                continue
            if w.get(s, 0) < v:
                waits.append((s, v))
                w[s] = v
        if dma_sem is None:
            self.count[eng] += 1
            sp = (eng, self.count[eng])
        else:
            self.dma_count[dma_sem] = self.dma_count.get(dma_sem, 0) + 16 * ndma
            sp = (dma_sem, self.dma_count[dma_sem])
            self.dma_last[dma_sem] = sp
        self.ops[eng].append((waits, fn, sp, dma_sem is not None))
        for k in reads:
            r = self.readers.setdefault(k, {})
            if r.get(sp[0], 0) < sp[1]:
                r[sp[0]] = sp[1]
        for k in writes:
            self.last_write[k] = sp
            self.readers[k] = {}
        return sp

    def final_wait(self, eng, sps):
        need = {}
        for s, v in sps:
            need[s] = max(need.get(s, 0), v)
        self.ops[eng].append((list(need.items()), None, None, False))

    def emit(self, nc):
        sems = {}
        names = list(self.ENG) + sorted(self.dma_count.keys())
        with ExitStack() as st:
            for n in names:
                sems[n] = st.enter_context(nc.semaphore("s_" + n))
            block = st.enter_context(nc.Block())

            def run(eng, e):
                for waits, fn, sp, is_dma in self.ops[eng]:
                    for s, v in waits:
                        e.wait_ge(sems[s], v)
                    if fn is None:
                        continue
                    r = fn(e)
                    if is_dma:
                        for ins in r:
                            ins.then_inc(sems[sp[0]], 16)
                    else:
                        r.then_inc(sems[eng], 1)

            @block.tensor
            def _(e):
                run("pe", e)

            @block.scalar
            def _(e):
                run("act", e)

            @block.vector
            def _(e):
                run("dve", e)

            @block.gpsimd
            def _(e):
                run("pool", e)

            @block.sync
            def _(e):
                run("sp", e)


def build(c):
    nc = bass.Bass("TRN2", target_bir_lowering=False)
    T, NB, KC, CG, SH, XC, XDC, MC, FC, NM = c.T, c.NB, c.KC, c.CG, c.SH, c.XC, c.XDC, c.MC, c.FC, c.NM
    D, WC, WS, WX, FF, DEPTH = c.D, c.WC, c.WS, c.WX, c.FF, c.DEPTH

    def dram(name, shape, dt=F32, kind="ExternalInput"):
        return nc.dram_tensor(name, list(shape), dt, kind=kind).ap()

    xsT = dram("xsT", [D, c.NTOK])
    memT = dram("memT", [D, NM])
    params_d = dram("params", [128, c.NP])
    cbs_d = dram("cbs", [128, DEPTH * SH * 128])
    sgwT_d = dram("sgwT", [DEPTH * 128, SH * 128])
    ident_d = dram("ident", [128, 128])
    zmask_d = dram("zmask", [128, T])
    w_in = dram("w_in", [DEPTH * D, c.WIN])
    w_kv = dram("w_mem_kv", [DEPTH * D, 2 * WX])
    w_out = dram("w_out", [DEPTH * D, D])
    w_gu = dram("w_gate_up", [DEPTH * D, 2 * FF])
    w_dn = dram("w_down", [DEPTH * FF, D])
    outT = dram("outT", [D, c.OWN], kind="ExternalOutput")
    KVW = XC * NM + MC * WX
    kvscr = dram("kvscr", [DEPTH * 128, KVW], BF16, kind="Internal")

    WNAME = {id(w_in): "w_in", id(w_kv): "w_kv", id(w_out): "w_out", id(w_gu): "w_gu", id(w_dn): "w_dn"}
    S = Sched()
    steps = []
    bg = deque()

    alloc_off = [0]
    bufs = []

    class Buf:
        def __init__(self, name, n, w, dt):
            self.name, self.n, self.w, self.dt = name, n, w, dt
            self.esz = 4 if dt == F32 else 2
            self.nbytes = n * w * self.esz
            assert self.nbytes % 4 == 0
            self.off = alloc_off[0]
            alloc_off[0] += (self.nbytes + GRAN - 1) // GRAN * GRAN
            bufs.append(self)

        def full(self):
            a = arena[:, self.off // 4:(self.off + self.nbytes) // 4]
            if self.dt != F32:
                a = a.bitcast(self.dt)
            return a

        def all(self):
            return V(self.full(), [("sb", g) for g in range(self.off // GRAN, (self.off + self.nbytes - 1) // GRAN + 1)])

        def v(self, i=0, c0=0, c1=None, n=1):
            w = self.w
            if c1 is None:
                c1 = w
            f = self.full()
            if n == 1:
                ap = f[:, i * w + c0:i * w + c1]
            else:
                ap = f[:, i * w:(i + n) * w].rearrange("p (n w) -> p n w", w=w)[:, :, c0:c1]
            b0 = self.off + (i * w + c0) * self.esz
            b1 = self.off + ((i + n - 1) * w + c1) * self.esz
            return V(ap, [("sb", g) for g in range(b0 // GRAN, (b1 - 1) // GRAN + 1)])

    ZW = CONVK - 1 + T
    xT = Buf("xT", KC, T, F32)
    bufA = Buf("bufA", KC, T, BF16)
    ring = Buf("ring", NSLOT, CAP, BF16)
    prm = Buf("prm", 1, c.NP, F32)
    zhalo = Buf("zhalo", DEPTH * CG, CONVK - 1 + 2, F32)
    kvbuf = Buf("kvbuf", 1, KVW, BF16)
    ones = Buf("ones", 1, 128, BF16)
    ident = Buf("ident", 1, 128, BF16)
    zmask = Buf("zmask", 1, T, F32)
    epsb = Buf("epsb", 1, 2, F32)
    zfull = Buf("zfull", CG, ZW, F32)
    sig = Buf("sig", 2, T, F32)
    acc = Buf("acc", 2, T, F32)
    cA = Buf("cA", CG, T, BF16)
    vt = Buf("vt", NB, WS, BF16)
    vtmp = Buf("vtmp", 2, WS, F32)
    vsq = Buf("vsq", 1, WS, F32)
    ub = Buf("ub", SH, T, BF16)
    stmp = Buf("stmp", 2, T, F32)
    WmT = Buf("WmT", SH, 128, BF16)
    Eb = Buf("Eb", SH, 128, F32)
    qT = Buf("qT", XC, T, BF16)
    oX = Buf("oX", XC, T, F32)
    yX = Buf("yX", XC, T, BF16)
    ebuf = Buf("ebuf", 2, NM, F32)
    pbuf = Buf("pbuf", 2, NM, BF16)
    pT = Buf("pT", 2 * MC, T, BF16)
    sqt = Buf("sqt", 2, T, BF16)
    cbt = Buf("cbt", 2, T, BF16)
    meanb = Buf("meanb", 1, T, F32)
    varb = Buf("varb", 1, T, F32)
    rstdb = Buf("rstdb", 1, T, F32)
    small = Buf("small", 1, 64, F32)
    outst = Buf("outst", 2, T, F32)
    actb = Buf("actb", 2 * 4, T, BF16)
    sgb = Buf("sgb", 4, T, F32)
    total_bytes = alloc_off[0]
    assert total_bytes <= 206 * 1024, total_bytes

    st = ExitStack()
    arena = st.enter_context(nc.sbuf_tensor("arena", [128, total_bytes // 4], F32))
    psum = st.enter_context(nc.psum_tensor("psum", [128, 8, 512], F32))

    def bank(b, n0=0, n1=None):
        if n1 is None:
            n1 = T
        return V(psum[:, b, n0:n1], [("ps", b)])

    def bank_bf(b, n0, n1):
        return V(psum[:, b, :].bitcast(BF16)[:, n0:n1], [("ps", b)])

    def P(l, name, i=0, n=1):
        o = c.poff[l][name] + i
        return prm.v(0, o, o + n)

    def act(out, in_, func, bias=None, scale=None):
        kw = {}
        r = list(in_.keys)
        if isinstance(bias, V):
            kw["bias"] = bias.ap
            r += bias.keys
        elif bias is not None:
            kw["bias"] = bias
        if isinstance(scale, V):
            kw["scale"] = scale.ap
            r += scale.keys
        elif scale is not None:
            kw["scale"] = scale
        S.op("act", lambda e: e.activation(out=out.ap, in_=in_.ap, func=func, **kw), r, out.keys)

    def tt(out, a, b, op):
        S.op("dve", lambda e: e.tensor_tensor(out=out.ap, in0=a.ap, in1=b.ap, op=op),
             a.keys + b.keys, out.keys)

    def ts(out, a, s1, op0, s2=None, op1=None):
        r = list(a.keys)
        a1 = s1
        a2 = s2
        if isinstance(s1, V):
            a1 = s1.ap
            r += s1.keys
        if isinstance(s2, V):
            a2 = s2.ap
            r += s2.keys
        if op1 is None:
            S.op("dve", lambda e: e.tensor_scalar(out=out.ap, in0=a.ap, scalar1=a1, scalar2=None, op0=op0),
                 r, out.keys)
        else:
            S.op("dve", lambda e: e.tensor_scalar(out=out.ap, in0=a.ap, scalar1=a1, scalar2=a2, op0=op0, op1=op1),
                 r, out.keys)

    def stt(out, a, s, b, op0, op1):
        r = list(a.keys) + list(b.keys)
        sa = s
        if isinstance(s, V):
            sa = s.ap
            r += s.keys
        S.op("dve", lambda e: e.scalar_tensor_tensor(out=out.ap, in0=a.ap, scalar=sa, in1=b.ap, op0=op0, op1=op1),
             r, out.keys)

    def vcopy(out, a):
        S.op("dve", lambda e: e.tensor_copy(out=out.ap, in_=a.ap), a.keys, out.keys)

    def recip(out, a):
        S.op("dve", lambda e: e.reciprocal(out=out.ap, in_=a.ap), a.keys, out.keys)

    def rmax(out, a):
        S.op("dve", lambda e: e.reduce_max(out=out.ap, in_=a.ap, axis=AX.X), a.keys, out.keys)

    def rsum(out, a):
        S.op("dve", lambda e: e.reduce_sum(out=out.ap, in_=a.ap, axis=AX.X), a.keys, out.keys)

    def memset(out, val):
        S.op("dve", lambda e: e.memset(out.ap, val), (), out.keys)

    def mm(group, reads, writes):
        def fn(e):
            ins = None
            for o, l, r, s0, s1 in group:
                ins = e.matmul(o, l, r, start=s0, stop=s1)
            return ins
        S.op("pe", fn, reads, writes)

    def transpose(out, in_):
        idv = ident.v()
        S.op("pe", lambda e: e.transpose(out.ap, in_.ap, idv.ap), in_.keys + idv.keys, out.keys)

    def dma(eng, sem, pairs, reads, writes):
        def fn(e):
            return [e.dma_start(out=o, in_=i) for o, i in pairs]
        return S.op(eng, fn, reads, writes, dma_sem=sem, ndma=len(pairs))

    units = []

    cur_tile = [None]
    scr_off = {}
    scr_tot = [0]
    scr_piece = [0]
    PIECE = 1000000
    wscr_box = []

    class Unit:
        def __init__(self, W2d, wname, row0, kn, runs):
            self.W, self.row0, self.kn, self.runs = W2d, row0, kn, runs
            self.ncols = sum(n for _, n in runs)
            self.nel = self.kn * self.ncols
            assert self.nel <= CAP
            self.idx = len(units)
            self.slot = self.idx % NSLOT
            self.tile = cur_tile[0]
            self.first = True
            if self.tile is not None:
                key = (wname, row0, kn, tuple(runs))
                self.first = key not in scr_off
                if self.first:
                    if scr_tot[0] + self.nel > PIECE:
                        scr_piece[0] += 1
                        scr_tot[0] = 0
                    scr_off[key] = (scr_piece[0], scr_tot[0])
                    scr_tot[0] += self.nel
                self.spiece, self.soff = scr_off[key]
            units.append(self)

        def view(self):
            f = ring.full()
            return f[:, self.slot * CAP:self.slot * CAP + self.nel].rearrange("p (k c) -> p k c", c=self.ncols)

        def keys(self):
            return ring.v(self.slot).keys

        def issue(self):
            flat = ring.full()[:, self.slot * CAP:self.slot * CAP + self.nel]
            if self.first:
                sv = self.view()
                pairs = []
                o = 0
                for c0, n in self.runs:
                    src = self.W[self.row0:self.row0 + self.kn * 128, c0:c0 + n].rearrange("(k p) c -> p k c", p=128)
                    pairs.append((sv[:, :, o:o + n], src))
                    o += n
                dma("pool", "ring%d" % self.slot, pairs, (), self.keys())
                if self.tile is not None:
                    dma("sp", "wst%d" % self.slot, [(wscr_box[self.spiece][:, self.soff:self.soff + self.nel], flat)],
                        self.keys(), [("wscr", self.spiece, self.soff)])
            else:
                dma("sp", "ring%d" % self.slot, [(flat, wscr_box[self.spiece][:, self.soff:self.soff + self.nel])],
                    [("wscr", self.spiece, self.soff)], self.keys())

    def add_step(fn, unit=None):
        steps.append((unit, fn))

    def drain_bg(n):
        while bg and n > 0:
            bg.popleft()()
            n -= 1

    BGRATE = [4]

    def proj(W2d, row_kc0, KCn, groups, rhs, evac, N, token_major=False, lhs_blocks=None):
        wname = W2d.name if hasattr(W2d, "name") else str(id(W2d))
        for gi, runs in enumerate(groups):
            ncols = sum(n for _, n in runs)
            nb = (ncols // 128) if not token_major else lhs_blocks
            nun = (KCn + UK - 1) // UK
            for u in range(nun):
                k0 = u * UK
                kn = min(UK, KCn - k0)
                un = Unit(W2d, WNAME[id(W2d)], (row_kc0 + k0) * 128, kn, runs)

                def fn(un=un, k0=k0, kn=kn, gi=gi, nb=nb, ncols=ncols, last=(u == nun - 1)):
                    sv = un.view()
                    for j in range(nb):
                        grp = []
                        rk = list(un.keys())
                        for kk in range(kn):
                            kc = k0 + kk
                            r = rhs(kc)
                            rk += r.keys
                            if not token_major:
                                grp.append((psum[:, j, 0:N], sv[:, kk, j * 128:(j + 1) * 128], r.ap,
                                            kc == 0, kc == KCn - 1))
                            else:
                                grp.append((psum[:, j, 0:ncols], r.ap[:, j * 128:(j + 1) * 128], sv[:, kk, :],
                                            kc == 0, kc == KCn - 1))
                        mm(grp, rk, [("ps", j)])
                        if last:
                            evac(gi, j, V(psum[:, j, 0:(N if not token_major else ncols)], [("ps", j)]))
                    drain_bg(BGRATE[0])
                add_step(fn, un)

    AUX = (4, 5)
    MISC = 6
    TRB = 7
    aux_i = [0]

    def next_aux():
        aux_i[0] ^= 1
        return AUX[aux_i[0]]

    def ones_sum(srcs, width, dst_rstd):
        b = next_aux()
        n = len(srcs)
        N = srcs[0].ap.shape[-1]
        for i, s in enumerate(srcs):
            q = sqt.v(i % 2, 0, N)
            act(q, s, AF.Square)
            mm([(psum[:, b, 0:N], ones.v().ap, q.ap, i == 0, i == n - 1)], q.keys + ones.v().keys, [("ps", b)])
        act(dst_rstd, V(psum[:, b, 0:N], [("ps", b)]), AF.Sqrt, bias=epsb.v(0, 0, 1), scale=1.0 / width)
        recip(dst_rstd, dst_rstd)

    def rms_apply(srcs, gains, dsts, width, rstd):
        ones_sum(srcs, width, rstd)
        for s, g, d in zip(srcs, gains, dsts):
            stt(d, s, g, rstd, ALU.mult, ALU.mult)

    dma("sp", "ld0", [(prm.v().ap, params_d[:, :])], (), prm.v().keys)
    dma("sp", "ld0", [(zmask.v().ap, zmask_d[:, :])], (), zmask.v().keys)
    dma("pool", "ld1", [(ident.v().ap, ident_d[:, :])], (), ident.v().keys)
    memset(ones.v(), 1.0)
    memset(epsb.v(), EPS)
    memset(zhalo.all(), 0.0)

    def load_cols(dst_buf, src2d, col0, ncol, sem):
        for k0 in range(0, KC, 8):
            kn = min(8, KC - k0)
            d = dst_buf.v(k0, 0, ncol, n=kn)
            src = src2d[k0 * 128:(k0 + kn) * 128, col0:col0 + ncol].rearrange("(k p) c -> p k c", p=128)
            dap = d.ap if kn > 1 else d.ap
            if kn == 1:
                src = src2d[k0 * 128:(k0 + 1) * 128, col0:col0 + ncol]
            dma("sp", sem, [(dap, src)], (), d.keys)

    def phase0():
        load_cols(xT, memT, 0, NM, "xld")
        for l in range(DEPTH):
            def f(l=l):
                rms_apply([xT.v(k) for k in range(KC)], [P(l, "mn", k) for k in range(KC)],
                          [bufA.v(k) for k in range(KC)], D, rstdb.v())
            add_step(f)
            kgroups = [[(g * 512, min(512, WX - g * 512))] for g in range((WX + 511) // 512)]

            def evk(gi, j, bv, l=l):
                xc = gi * 4 + j
                act(kvbuf.v(0, xc * NM, (xc + 1) * NM), bv, AF.Copy)
            proj(w_kv, l * KC, KC, kgroups, lambda kc: bufA.v(kc), evk, NM)
            vgroups = [[(WX + g * 512, min(512, WX - g * 512))] for g in range((WX + 511) // 512)]

            def evv(gi, b, bv, l=l):
                o = XC * NM + b * WX + gi * 512
                act(kvbuf.v(0, o, o + bv.ap.shape[-1]), bv, AF.Copy)
            proj(w_kv, l * KC, KC, vgroups, lambda kc: bufA.v(kc), evv, NM, token_major=True, lhs_blocks=MC)

            def st_(l=l):
                dma("sp", "kvst", [(kvscr[l * 128:(l + 1) * 128, :], kvbuf.v().ap)], kvbuf.v().keys, [("kvscr", l)])
            add_step(st_)

    def layer(t, l, last_tile_layer):
        first_tile = (t == 0)

        def prologue():
            dma("sp", "kvld", [(kvbuf.v().ap, kvscr[l * 128:(l + 1) * 128, :])], [("kvscr", l)], kvbuf.v().keys)
            dma("pool", "ld1", [(WmT.all().ap, sgwT_d[l * 128:(l + 1) * 128, :])], (), WmT.all().keys)
            dma("sp", "ld0", [(Eb.all().ap, cbs_d[:, l * SH * 128:(l + 1) * SH * 128])], (), Eb.all().keys)
            wv = WmT.v(0, 0, 64, n=SH)
            S.op("dve", lambda e: e.memset(wv.ap[64:128], 0.0), (), wv.keys)
            for h0 in range(0, SH, 4):
                hn = min(4, SH - h0)
                wm = WmT.v(h0, n=hn)
                mm([(psum[:, MISC, 0:hn * 128], ones.v().ap, WmT.full()[:, h0 * 128:(h0 + hn) * 128], True, True)],
                   wm.keys + ones.v().keys, [("ps", MISC)])
                for h in range(h0, h0 + hn):
                    stt(Eb.v(h), V(psum[:, MISC, (h - h0) * 128:(h - h0 + 1) * 128], [("ps", MISC)]),
                        P(l, "sb", h), Eb.v(h), ALU.mult, ALU.add)
            rms_apply([xT.v(k) for k in range(KC)], [P(l, "nm", k) for k in range(KC)],
                      [bufA.v(k) for k in range(KC)], D, rstdb.v())
        add_step(prologue)

        hrhs = lambda kc: bufA.v(kc)

        def conv_chunk_ops(cc):
            zf = lambda a, b: zfull.v(cc, a, b)
            a_ = acc.v(cc % 2)
            ops = []
            ops.append(lambda: vcopy(zf(0, CONVK - 1), zhalo.v(l * CG + cc, 0, CONVK - 1)))
            ops.append(lambda: ts(a_, zf(0, T), P(l, "cw", 0 * CG + cc), ALU.mult, P(l, "cb", cc), ALU.add))
            for k in range(1, CONVK):
                ops.append(lambda k=k: stt(a_, zf(k, k + T), P(l, "cw", k * CG + cc), a_, ALU.mult, ALU.add))
            ops.append(lambda: vcopy(zhalo.v(l * CG + cc, 0, CONVK - 1), zf(T, T + CONVK - 1)))
            ops.append(lambda: act(cA.v(cc), a_, AF.Copy))
            return ops

        def evA(gi, j, bv):
            if j < 2:
                return
            cc = 2 * gi + (j - 2)
            sg_ = sig.v(cc % 2)
            act(sg_, bv, AF.Sigmoid)
            zt = zfull.v(cc, CONVK - 1, CONVK - 1 + T)
            tt(zt, V(psum[:, j - 2, 0:T], [("ps", j - 2)]), sg_, ALU.mult)
            if first_tile:
                tt(zt, zt, zmask.v(), ALU.mult)
            if cc % 2 == 1:
                o0 = conv_chunk_ops(cc - 1)
                o1 = conv_chunk_ops(cc)
                for a, b in zip(o0, o1):
                    bg.append(a)
                    bg.append(b)

        agroups = [[(g * 256, 256), (WC + g * 256, 256)] for g in range(CG // 2)]
        proj(w_in, l * KC, KC, agroups, hrhs, evA, T)
        if (t + 1) * T <= c.HALO and l == DEPTH - 1:
            add_step(lambda: drain_bg(1 << 30))
            return

        def evGv(gi, b, bv):
            o = gi * 512
            w = bv.ap.shape[-1]
            act(vtmp.v(b % 2, o, o + w), bv, AF.Gelu)
            if o + w == WS:
                def lnv(b=b):
                    src = vtmp.v(b % 2)
                    s1 = small.v(0, 0 + b * 8, 1 + b * 8)
                    s2 = small.v(0, 1 + b * 8, 2 + b * 8)
                    m_ = small.v(0, 2 + b * 8, 3 + b * 8)
                    v_ = small.v(0, 3 + b * 8, 4 + b * 8)
                    rsum(s1, src)
                    ts(m_, s1, 1.0 / WS, ALU.mult)
                    ts(src, src, m_, ALU.subtract)
                    tt(vsq.v(), src, src, ALU.mult)
                    rsum(s2, vsq.v())
                    ts(v_, s2, 1.0 / WS, ALU.mult, EPS, ALU.add)
                    act(v_, v_, AF.Sqrt)
                    recip(v_, v_)
                    ts(vt.v(b), src, v_, ALU.mult)
                bg.append(lnv)

        vgroups = [[(2 * WC + WS + g * 512, min(512, WS - g * 512))] for g in range((WS + 511) // 512)]
        proj(w_in, l * KC, KC, vgroups, hrhs, evGv, T, token_major=True, lhs_blocks=NB)

        def evGu(gi, j, bv):
            h = gi * 4 + j
            act(ub.v(h), bv, AF.Gelu)

            def mix(h=h):
                grp = []
                rk = list(WmT.v(h).keys)
                for b in range(NB):
                    nv = vt.v(b, h * 128, (h + 1) * 128)
                    rk += nv.keys
                    grp.append((psum[:, MISC, b * 128:(b + 1) * 128], nv.ap, WmT.v(h).ap, True, True))
                mm(grp, rk, [("ps", MISC)])
                s_ = stmp.v(h % 2)
                for b in range(NB):
                    stt(stmp.v(h % 2, b * 128, (b + 1) * 128),
                        V(psum[:, MISC, b * 128:(b + 1) * 128], [("ps", MISC)]),
                        P(l, "sg", h), Eb.v(h), ALU.mult, ALU.add)
                tt(ub.v(h), ub.v(h), s_, ALU.mult)
            bg.append(mix)

        ugroups = [[(2 * WC + g * 512, min(512, WS - g * 512))] for g in range((WS + 511) // 512)]
        proj(w_in, l * KC, KC, ugroups, hrhs, evGu, T)

        def a_post():
            drain_bg(1 << 30)
            b1 = next_aux()
            b2 = next_aux()
            for cc in range(CG):
                q = sqt.v(cc % 2)
                act(q, cA.v(cc), AF.Square)
                mm([(psum[:, b1, 0:T], ones.v().ap, cA.v(cc).ap, cc == 0, cc == CG - 1)],
                   cA.v(cc).keys + ones.v().keys, [("ps", b1)])
                mm([(psum[:, b2, 0:T], ones.v().ap, q.ap, cc == 0, cc == CG - 1)],
                   q.keys + ones.v().keys, [("ps", b2)])
            ts(meanb.v(), bank(b1), 1.0 / WC, ALU.mult)
            tt(varb.v(), meanb.v(), meanb.v(), ALU.mult)
            stt(varb.v(), bank(b2), 1.0 / WC, varb.v(), ALU.mult, ALU.subtract)
            act(varb.v(), varb.v(), AF.Sqrt, bias=epsb.v(0, 0, 1), scale=1.0)
            recip(varb.v(), varb.v())
            for cc in range(CG):
                a_ = acc.v(cc % 2)
                tt(a_, cA.v(cc), meanb.v(), ALU.subtract)
                tt(a_, a_, varb.v(), ALU.mult)
                act(a_, a_, AF.Silu, bias=P(l, "clb", cc), scale=P(l, "clg", cc))
                act(cA.v(cc), a_, AF.Copy)
            rms_apply([cA.v(cc) for cc in range(CG)], [P(l, "on", cc) for cc in range(CG)],
                      [cA.v(cc) for cc in range(CG)], WC, rstdb.v())
        add_step(a_post)

        def ev_res(gi, j, bv):
            m = gi * 4 + j
            tt(xT.v(m), xT.v(m), bv, ALU.add)

        ogroups = [[(g * 512, min(512, D - g * 512))] for g in range((D + 511) // 512)]
        proj(w_out, l * KC, CG, ogroups, lambda kc: cA.v(kc), ev_res, T)

        scale = float((WX // c.XH) ** -0.5)

        def evQ(gi, j, bv):
            xc = gi * 4 + j
            act(qT.v(xc), bv, AF.Copy)
            if (xc + 1) % XDC == 0:
                h = xc // XDC
                for b in range(NB):
                    def att(h=h, b=b):
                        grp = []
                        rk = list(kvbuf.v().keys)
                        for d_ in range(XDC):
                            q_ = qT.v(h * XDC + d_, b * 128, (b + 1) * 128)
                            rk += q_.keys
                            kv_ = kvbuf.v(0, (h * XDC + d_) * NM, (h * XDC + d_ + 1) * NM)
                            grp.append((psum[:, MISC, 0:NM], q_.ap, kv_.ap, d_ == 0, d_ == XDC - 1))
                        mm(grp, rk, [("ps", MISC)])
                        sc = V(psum[:, MISC, 0:NM], [("ps", MISC)])
                        i_ = (h * NB + b) % 2
                        mx = small.v(0, 32 + i_ * 4, 33 + i_ * 4)
                        sm = small.v(0, 34 + i_ * 4, 35 + i_ * 4)
                        rmax(mx, sc)
                        ts(mx, mx, -scale, ALU.mult)
                        e_ = ebuf.v(i_)
                        act(e_, sc, AF.Exp, bias=mx, scale=scale)
                        rsum(sm, e_)
                        recip(sm, sm)
                        p_ = pbuf.v(i_)
                        ts(p_, e_, sm, ALU.mult)
                        for mc in range(MC):
                            tb = bank_bf(TRB, mc * 128, (mc + 1) * 128)
                            transpose(tb, pbuf.v(i_, mc * 128, (mc + 1) * 128))
                            act(pT.v((h % 2) * MC + mc, b * 128, (b + 1) * 128), tb, AF.Copy)
                    bg.append(att)

                def av(h=h):
                    for d_ in range(XDC):
                        grp = []
                        rk = list(kvbuf.v().keys)
                        bk = AUX[d_ % 2]
                        for mc in range(MC):
                            p_ = pT.v((h % 2) * MC + mc)
                            rk += p_.keys
                            o = XC * NM + mc * WX + (h * XDC + d_) * 128
                            grp.append((psum[:, bk, 0:T], kvbuf.v(0, o, o + 128).ap, p_.ap, mc == 0, mc == MC - 1))
                        mm(grp, rk, [("ps", bk)])
                        act(oX.v(h * XDC + d_), bank(bk), AF.Copy)
                bg.append(av)

        qgroups = [[(2 * WC + 2 * WS + g * 512, min(512, WX - g * 512))] for g in range((WX + 511) // 512)]
        proj(w_in, l * KC, KC, qgroups, hrhs, evQ, T)

        def g_post():
            drain_bg(1 << 30)
            rms_apply([ub.v(h) for h in range(SH)], [P(l, "on", CG + h) for h in range(SH)],
                      [ub.v(h) for h in range(SH)], WS, rstdb.v())
        add_step(g_post)
        proj(w_out, l * KC + CG, SH, ogroups, lambda kc: ub.v(kc), ev_res, T)

        def x_post():
            drain_bg(1 << 30)
            rms_apply([oX.v(k) for k in range(XC)], [P(l, "on", CG + SH + k) for k in range(XC)],
                      [yX.v(k) for k in range(XC)], WX, rstdb.v())
        add_step(x_post)
        proj(w_out, l * KC + CG + SH, XC, ogroups, lambda kc: yX.v(kc), ev_res, T)

        def ffn_pro():
            rms_apply([xT.v(k) for k in range(KC)], [P(l, "nf", k) for k in range(KC)],
                      [bufA.v(k) for k in range(KC)], D, rstdb.v())
        add_step(ffn_pro)
        nfg = (FC + 3) // 4
        for fg in range(nfg):
            f0 = fg * 4
            fn_ = min(4, FC - f0)
            half = fg % 2

            def evg(gi, j, bv):
                act(sgb.v(j), bv, AF.Silu)

            def evu(gi, j, bv, half=half):
                tt(actb.v(half * 4 + j), bv, sgb.v(j), ALU.mult)
            proj(w_gu, l * KC, KC, [[(f0 * 128, fn_ * 128)]], hrhs, evg, T)
            proj(w_gu, l * KC, KC, [[(FF + f0 * 128, fn_ * 128)]], hrhs, evu, T)
            proj(w_dn, l * FC + f0, fn_, ogroups, lambda kc, half=half: actb.v(half * 4 + kc), ev_res, T)

        if last_tile_layer:
            def fin():
                ones_sum([xT.v(k) for k in range(KC)], D, rstdb.v())
                c0 = t * T - c.HALO
                for k in range(KC):
                    o_ = outst.v(k % 2)
                    stt(o_, xT.v(k), prm.v(0, c.fn_off + k, c.fn_off + k + 1), rstdb.v(), ALU.mult, ALU.mult)
                    sp = dma("sp", "ost%d" % (k % 2), [(outT[k * 128:(k + 1) * 128, c0:c0 + T], o_.ap)], o_.keys, ())
                    out_sps.append(sp)
            add_step(fin)

    out_sps = []

    phase0()
    for t in range(c.NT):
        cur_tile[0] = t

        def ld(t=t):
            load_cols(xT, xsT, t * T, T, "xld")
        add_step(ld)
        for l in range(DEPTH):
            layer(t, l, (l == DEPTH - 1) and (t * T >= c.HALO))

    for i_ in range(scr_piece[0] + 1):
        wscr_box.append(dram("wscr%d" % i_, [128, PIECE], BF16, kind="Internal"))
    issued = 0
    for un, fn in steps:
        if un is not None:
            while issued < len(units) and issued <= un.idx + NSLOT - 1:
                units[issued].issue()
                issued += 1
        fn()
    drain_bg(1 << 30)
    S.final_wait("sp", out_sps)
    S.emit(nc)
    st.close()
    return nc


def host_inputs(c, inp):
    D, T = c.D, c.T
    x = np.asarray(inp["x"], np.float32)[0]
    mem = np.asarray(inp["mem"], np.float32)[0]

    def pm(v):
        return np.ascontiguousarray(np.asarray(v, np.float32).reshape(-1, 128).T)
    params = np.zeros((128, c.NP), np.float32)
    for l in range(c.DEPTH):
        o = c.poff[l]
        params[:, o["nm"]:o["nm"] + c.KC] = pm(inp["norm_mix"][l])
        params[:, o["on"]:o["on"] + c.KC] = pm(inp["out_norm"][l])
        params[:, o["nf"]:o["nf"] + c.KC] = pm(inp["norm_ffn"][l])
        params[:, o["mn"]:o["mn"] + c.KC] = pm(inp["mem_norm"][l])
        cw = np.asarray(inp["conv_w"][l], np.float32)
        for k in range(CONVK):
            params[:, o["cw"] + k * c.CG:o["cw"] + (k + 1) * c.CG] = pm(cw[k])
        params[:, o["cb"]:o["cb"] + c.CG] = pm(inp["conv_b"][l])
        params[:, o["clg"]:o["clg"] + c.CG] = pm(inp["conv_ln_g"][l])
        params[:, o["clb"]:o["clb"] + c.CG] = pm(inp["conv_ln_b"][l])
        params[:, o["sg"]:o["sg"] + c.SH] = pm(inp["sg_ln_g"][l])
        params[:, o["sb"]:o["sb"] + c.SH] = pm(inp["sg_ln_b"][l])
    params[:, c.fn_off:c.fn_off + c.KC] = pm(inp["final_norm"])
    sgb = np.asarray(inp["sg_b"], np.float32).reshape(1, -1)
    cbs = np.ascontiguousarray(np.broadcast_to(sgb, (128, sgb.shape[1])))
    sgw = np.asarray(inp["sg_w"], np.float32)
    sgwT = np.ascontiguousarray(sgw.transpose(0, 3, 1, 2)).reshape(c.DEPTH * 128, c.SH * 128)
    ident = np.eye(128, dtype=np.float32)
    memT = np.ascontiguousarray(mem.T)
    shared = {
        "memT": memT, "params": params, "cbs": cbs, "sgwT": sgwT, "ident": ident,
        "w_in": np.asarray(inp["w_in"], np.float32).reshape(c.DEPTH * D, c.WIN),
        "w_mem_kv": np.asarray(inp["w_mem_kv"], np.float32).reshape(c.DEPTH * D, 2 * c.WX),
        "w_out": np.asarray(inp["w_out"], np.float32).reshape(c.DEPTH * D, D),
        "w_gate_up": np.asarray(inp["w_gate_up"], np.float32).reshape(c.DEPTH * D, 2 * c.FF),
        "w_down": np.asarray(inp["w_down"], np.float32).reshape(c.DEPTH * c.FF, D),
    }
    maps = []
    for core in range(c.NCORES):
        s = core * c.OWN
        xs = np.zeros((c.NTOK, D), np.float32)
        lo = s - c.HALO
        if lo >= 0:
            xs[:] = x[lo:s + c.OWN]
        else:
            xs[c.HALO:] = x[s:s + c.OWN]
        zm = np.ones((128, T), np.float32)
        if core == 0:
            zm[:] = 0.0
        m = dict(shared)
        m["xsT"] = np.ascontiguousarray(xs.T)
        m["zmask"] = zm
        maps.append(m)
    return maps


def run(c, inp):
    nc = build(c)
    maps = host_inputs(c, inp)
    res = run_bass_kernel_spmd(nc, maps, core_ids=list(range(c.NCORES)))
    outs = [np.asarray(r["outT"], np.float32).T for r in res.results]
    return np.concatenate(outs, axis=0)[None]


def kernel(**inputs):
    return run(FULL, inputs)
```
